# Optimizing a Trainium2 kernel written in Bass

```python
import math
import jax, jax.numpy as jnp
from jax import lax
import numpy as np

D_MODEL = 1024
BATCH = 8
SEQ = 4096
DEPTH = 1

CHUNK = 64
D_MIX = D_MODEL
D_A = D_MIX // 2
D_B = D_MIX - D_A
H_A = 8
DH_A = D_A // H_A
H_B = 8
DH_B = D_B // H_B
SGU_BLOCK = 128
Q_BLOCK = 128
D_FF = 2816
N_SUB = 3
OFF_Z_A = 0
OFF_Q = 2 * D_A
OFF_K = OFF_Q + D_B
OFF_V = OFF_K + D_B
OFF_F = OFF_V + D_B
N_IN = OFF_F + H_B
EPS = 1e-6
NEG_INF = -1e30

kernel_name = 'hybrid_sgu_fox_macaron_block'


def rms_norm(x, g):
    xf = x.astype(jnp.float32)
    y = xf * lax.rsqrt(jnp.mean(xf * xf, axis=-1, keepdims=True) + EPS)
    return (y * g.astype(jnp.float32)).astype(x.dtype)


def layer_norm(x, g, b):
    xf = x.astype(jnp.float32)
    mu = jnp.mean(xf, axis=-1, keepdims=True)
    xc = xf - mu
    y = xc * lax.rsqrt(jnp.mean(xc * xc, axis=-1, keepdims=True) + EPS)
    return (y * g.astype(jnp.float32) + b.astype(jnp.float32)).astype(x.dtype)


def modulate(x, g_pre, shift, scale):
    return rms_norm(x, g_pre) * (1 + scale[:, None, :]) + shift[:, None, :]


def swiglu(h, w_gate, w_up, w_down):
    return (jax.nn.silu(h @ w_gate) * (h @ w_up)) @ w_down


def spatial_gating(z, ln_g, ln_b, w_s, b_s):
    b_, s_, _ = z.shape
    z = jax.nn.gelu(z)
    u, v = z[..., :D_A], z[..., D_A:]
    v = layer_norm(v, ln_g, ln_b)
    v = v.reshape(b_, s_ // SGU_BLOCK, SGU_BLOCK, H_A, DH_A)
    pos = jnp.arange(SGU_BLOCK)
    mask = (pos[None, :] // CHUNK) <= (pos[:, None] // CHUNK)
    w = jnp.where(mask[None], w_s, jnp.zeros((), w_s.dtype))
    gate = jnp.einsum('hij,bnjhc->bnihc', w, v) + b_s.T[None, None, :, :, None]
    return u * gate.reshape(b_, s_, D_A)


def forgetting_attention(q, k, v, log_f):
    s_ = q.shape[2]
    cum = jnp.cumsum(log_f, axis=-1)
    scale = DH_B ** -0.5
    outs = []
    for blk in range(s_ // Q_BLOCK):
        q0 = blk * Q_BLOCK
        q1 = q0 + Q_BLOCK
        logits = jnp.einsum('bhqd,bhkd->bhqk', q[:, :, q0:q1], k[:, :, :q1]).astype(jnp.float32) * scale
        logits = logits + cum[:, :, q0:q1, None] - cum[:, :, None, :q1]
        qpos = q0 + jnp.arange(Q_BLOCK)
        kpos = jnp.arange(q1)
        logits = jnp.where(kpos[None, :] <= qpos[:, None], logits, NEG_INF)
        p = jax.nn.softmax(logits, axis=-1).astype(v.dtype)
        outs.append(jnp.einsum('bhqk,bhkd->bhqd', p, v[:, :, :q1]))
    return jnp.concatenate(outs, axis=2)


def hybrid_mixer(h, w_in, sgu_ln_g, sgu_ln_b, sgu_w, sgu_b, fox_b_f, gnorm_a_g, gnorm_b_g, w_out):
    b_, s_, _ = h.shape
    proj = h @ w_in
    y_a = spatial_gating(proj[..., OFF_Z_A:OFF_Q], sgu_ln_g, sgu_ln_b, sgu_w, sgu_b)

    def heads(t):
        return t.reshape(b_, s_, H_B, DH_B).transpose(0, 2, 1, 3)

    q = heads(proj[..., OFF_Q:OFF_K])
    k = heads(proj[..., OFF_K:OFF_V])
    v = heads(proj[..., OFF_V:OFF_F])
    log_f = jax.nn.log_sigmoid(proj[..., OFF_F:N_IN].astype(jnp.float32)
                               + fox_b_f.astype(jnp.float32)).transpose(0, 2, 1)
    y_b = forgetting_attention(q, k, v, log_f).transpose(0, 2, 1, 3).reshape(b_, s_, D_B)
    y = jnp.concatenate([rms_norm(y_a, gnorm_a_g), rms_norm(y_b, gnorm_b_g)], axis=-1)
    return y @ w_out


def setup_inputs(seed: int = 0) -> dict:
    key = jax.random.key(seed)
    ks = jax.random.split(key, 20)
    f32 = jnp.float32
    n = lambda k, shape, s: jax.random.normal(k, shape, f32) * s
    return {
        'x': n(ks[0], (BATCH, SEQ, D_MODEL), 1.0),
        'c': n(ks[1], (BATCH, D_MODEL), 1.0),
        'w_ada': n(ks[2], (DEPTH, D_MODEL, N_SUB * 3 * D_MODEL), 0.1 * D_MODEL ** -0.5),
        'b_ada': n(ks[3], (DEPTH, N_SUB * 3 * D_MODEL), 0.01),
        'norm_pre_g': 1.0 + n(ks[4], (DEPTH, N_SUB, D_MODEL), 0.02),
        'norm_post_g': 1.0 + n(ks[5], (DEPTH, N_SUB, D_MODEL), 0.02),
        'ffn_w_gate': n(ks[6], (DEPTH, 2, D_MODEL, D_FF), D_MODEL ** -0.5),
        'ffn_w_up': n(ks[7], (DEPTH, 2, D_MODEL, D_FF), D_MODEL ** -0.5),
        'ffn_w_down': n(ks[8], (DEPTH, 2, D_FF, D_MODEL), D_FF ** -0.5),
        'w_in': n(ks[9], (DEPTH, D_MODEL, N_IN), D_MODEL ** -0.5),
        'sgu_ln_g': 1.0 + n(ks[10], (DEPTH, D_A), 0.02),
        'sgu_ln_b': n(ks[11], (DEPTH, D_A), 0.02),
        'sgu_w': n(ks[12], (DEPTH, H_A, SGU_BLOCK, SGU_BLOCK), SGU_BLOCK ** -0.5),
        'sgu_b': 1.0 + n(ks[13], (DEPTH, H_A, SGU_BLOCK), 0.02),
        'fox_b_f': 2.0 + n(ks[14], (DEPTH, H_B), 0.5),
        'gnorm_a_g': 1.0 + n(ks[15], (DEPTH, D_A), 0.02),
        'gnorm_b_g': 1.0 + n(ks[16], (DEPTH, D_B), 0.02),
        'w_out': n(ks[17], (DEPTH, D_MIX, D_MODEL), D_MIX ** -0.5),
    }


def reference(x, c, w_ada, b_ada, norm_pre_g, norm_post_g, ffn_w_gate, ffn_w_up, ffn_w_down,
              w_in, sgu_ln_g, sgu_ln_b, sgu_w, sgu_b, fox_b_f, gnorm_a_g, gnorm_b_g, w_out):
    b_ = x.shape[0]
    for l in range(DEPTH):
        mod = (jax.nn.silu(c) @ w_ada[l] + b_ada[l]).reshape(b_, N_SUB, 3, D_MODEL)
        shift, scale, gate = mod[:, :, 0], mod[:, :, 1], mod[:, :, 2]

        h = modulate(x, norm_pre_g[l, 0], shift[:, 0], scale[:, 0])
        h = swiglu(h, ffn_w_gate[l, 0], ffn_w_up[l, 0], ffn_w_down[l, 0])
        x = x + 0.5 * (1 + gate[:, 0, None, :]) * rms_norm(h, norm_post_g[l, 0])

        h = modulate(x, norm_pre_g[l, 1], shift[:, 1], scale[:, 1])
        h = hybrid_mixer(h, w_in[l], sgu_ln_g[l], sgu_ln_b[l], sgu_w[l], sgu_b[l], fox_b_f[l],
                         gnorm_a_g[l], gnorm_b_g[l], w_out[l])
        x = x + (1 + gate[:, 1, None, :]) * rms_norm(h, norm_post_g[l, 1])

        h = modulate(x, norm_pre_g[l, 2], shift[:, 2], scale[:, 2])
        h = swiglu(h, ffn_w_gate[l, 1], ffn_w_up[l, 1], ffn_w_down[l, 1])
        x = x + 0.5 * (1 + gate[:, 2, None, :]) * rms_norm(h, norm_post_g[l, 2])
    return x
```

```python
import numpy as np
import concourse.bass as bass
import concourse.mybir as mybir

F32 = mybir.dt.float32
BF16 = mybir.dt.bfloat16
ALU = mybir.AluOpType
AF = mybir.ActivationFunctionType
AX = mybir.AxisListType

ENGS = ("pe", "act", "dve", "pool", "sp")


class Op:
    __slots__ = ("eng", "fn", "deps", "signal", "semval", "is_dma", "dkey", "idx", "gidx")

    def __init__(self, eng, fn, is_dma=False, dkey=None):
        self.eng = eng
        self.fn = fn
        self.deps = []
        self.signal = False
        self.semval = 0
        self.is_dma = is_dma
        self.dkey = dkey
        self.idx = -1
        self.gidx = -1


class Prog:
    def __init__(self, nc):
        self.nc = nc
        self.ops = {e: [] for e in ENGS}
        self.last_w = {}
        self.readers = {}
        self.dma_cnt = {}
        self.n = 0

    def _add(self, op, reads, writes):
        op.gidx = self.n
        self.n += 1
        deps = []
        for r in reads:
            w = self.last_w.get(r)
            if w is not None:
                deps.append((w, "raw"))
        for r in writes:
            w = self.last_w.get(r)
            if w is not None:
                deps.append((w, "waw"))
            for rd in self.readers.get(r, ()):
                deps.append((rd, "war"))
        seen = set()
        for d, kind in deps:
            if d is op or id(d) in seen:
                continue
            if d.eng == op.eng and not d.is_dma and not op.is_dma:
                if op.eng == "pe":
                    continue
            if d.eng == op.eng and d.is_dma and op.is_dma and kind == "waw" and d.dkey == op.dkey:
                continue
            seen.add(id(d))
            op.deps.append(d)
            d.signal = True
        for r in reads:
            self.readers.setdefault(r, []).append(op)
        for r in writes:
            self.last_w[r] = op
            self.readers[r] = []
        self.ops[op.eng].append(op)
        return op

    def op(self, eng, fn, reads=(), writes=()):
        return self._add(Op(eng, fn), reads, writes)

    def dma(self, queue, dkey, fn, reads=(), writes=()):
        o = Op(queue, fn, is_dma=True, dkey=dkey)
        o.signal = True
        return self._add(o, reads, writes)

    def emit(self, final_waits=()):
        nc = self.nc
        for e in ENGS:
            c = 0
            for o in self.ops[e]:
                if o.is_dma:
                    self.dma_cnt[o.dkey] = self.dma_cnt.get(o.dkey, 0) + 16
                    o.semval = self.dma_cnt[o.dkey]
                elif o.signal:
                    c += 1
                    o.semval = c
        dkeys = sorted(self.dma_cnt.keys(), key=str)
        import contextlib
        with contextlib.ExitStack() as st:
            esem = {e: st.enter_context(nc.semaphore("s_" + e)) for e in ENGS}
            dsem = {k: st.enter_context(nc.semaphore("d_%d" % i)) for i, k in enumerate(dkeys)}
            block = st.enter_context(nc.Block())
            engobj = {}

            def run(ename, eng):
                waited = {}
                for o in self.ops[ename]:
                    for d in o.deps:
                        if d.is_dma:
                            key = ("d", d.dkey)
                            sem = dsem[d.dkey]
                        else:
                            key = ("e", d.eng)
                            sem = esem[d.eng]
                        if waited.get(key, 0) >= d.semval:
                            continue
                        waited[key] = d.semval
                        eng.wait_ge(sem, d.semval)
                    ins = o.fn(eng)
                    if o.is_dma:
                        ins.then_inc(dsem[o.dkey], 16)
                    elif o.signal:
                        ins.then_inc(esem[ename], 1)
                if ename == "sp":
                    for o in final_waits:
                        if o.is_dma:
                            eng.wait_ge(dsem[o.dkey], o.semval)
                        else:
                            eng.wait_ge(esem[o.eng], o.semval)

            @block.tensor
            def _(eng):
                run("pe", eng)

            @block.scalar
            def _(eng):
                run("act", eng)

            @block.vector
            def _(eng):
                run("dve", eng)

            @block.gpsimd
            def _(eng):
                run("pool", eng)

            @block.sync
            def _(eng):
                run("sp", eng)


from contextlib import ExitStack
from concourse.bass_utils import run_bass_kernel_spmd

D = 1024
S = 4096
T = 512
NT = S // T
FF = 2816
NFC = FF // 128
NIN = 2568
EPS = 1e-6
NRING = 5
NTILES_RUN = NT


def build_program(ntiles=NT):
    nc = bass.Bass("TRN2", target_bir_lowering=False)
    dt_in = lambda name, shape: nc.dram_tensor(name, shape, F32, kind="ExternalInput").ap()
    x_d = dt_in("x", [S, D])
    out_d = nc.dram_tensor("out", [S, D], F32, kind="ExternalOutput").ap()
    c_col_d = dt_in("c_col", [128, 8])
    w_ada_d = dt_in("w_ada", [D, 9 * D])
    b_ada_d = dt_in("b_ada_col", [128, 72])
    gpre_d = dt_in("gpre_col", [128, 24])
    gpost_d = dt_in("gpost_col", [128, 24])
    wg_d = dt_in("wg", [2, D, FF])
    wu_d = dt_in("wu", [2, D, FF])
    wd_d = dt_in("wd", [2, FF, D])
    win_d = dt_in("w_in", [D, NIN])
    wout_d = dt_in("w_out", [D, D])
    bc_d = dt_in("bc4", [128, 4, 512])
    sguwT_d = dt_in("sguwT", [128, 8, 128])
    sgub_d = dt_in("sgub_col", [128, 8])
    bf_d = dt_in("bf_bc", [128, 8])

    with ExitStack() as st:
        def sb(name, shape, dt):
            return st.enter_context(nc.sbuf_tensor(name, shape, dt))

        def ps(name, shape, dt):
            return st.enter_context(nc.psum_tensor(name, shape, dt))

        KT = sb("KT", [128, 4, S], BF16)
        VA = sb("VA", [128, S // 128, 8, 65], BF16)
        xt = sb("xt", [128, 4, D], F32)
        tmp = sb("tmp", [128, 2, D], F32)
        xn = sb("xn", [128, 4, D], BF16)
        hT = sb("hT", [128, 8, T], BF16)
        hid = sb("hid", [128, NFC, T], BF16)
        ring = [sb("ring%d" % i, [128, 2048], BF16) for i in range(NRING)]
        Gt = sb("Gt", [128, 3, D], F32)
        sg = sb("sg", [128, 2, T], BF16)
        zuv = sb("zuv", [128, 2, D], F32)
        t12 = sb("t12", [128, 2, 512], F32)
        vnb = sb("vnb", [128, 2, 512], BF16)
        QT = sb("QT", [128, 8, T], BF16)
        PT = sb("PT", [128, 3, T], BF16)
        OTs = sb("OTs", [128, 2, T], F32)
        wmT = sb("wmT", [128, 8, 128], BF16)
        bc4 = sb("bc4s", [128, 4, 512], F32)
        ident = sb("ident", [128, 128], BF16)
        identf = sb("identf", [128, 128], F32)
        onesf = sb("onesf", [128, 128], F32)
        mask01 = sb("mask01", [128, 128], BF16)
        utri = sb("utri", [128, 128], F32)
        sel127 = sb("sel127", [128, 128], F32)
        sel63 = sb("sel63", [128, 128], F32)
        mhalf = sb("mhalf", [128, 1], F32)
        ccol = sb("ccol", [128, 8], F32)
        scb = sb("scb", [128, 8], BF16)
        bada = sb("bada", [128, 72], F32)
        modT = sb("modT", [128, 72], F32)
        gpre = sb("gpre", [128, 24], F32)
        gpost = sb("gpost", [128, 24], F32)
        Acol = sb("Acol", [128, 24], F32)
        gcol = sb("gcol", [128, 24], F32)
        sgub = sb("sgub", [128, 8], F32)
        bfbc = sb("bfbc", [128, 8], F32)
        stt = sb("stt", [128, 96], F32)
        cn = sb("cn", [128, S // 128, 8], F32)
        roff = sb("roff", [128, 2, T], BF16)
        onesb = sb("onesb", [128, 128], BF16)
        fl = sb("fl", [128, 3, 32], F32)
        bst = sb("bst", [128, 2, 6], F32)
        mv = sb("mv", [128, 2, 2], F32)
        rl = sb("rl", [128, 2, 4], F32)
        sguw = tmp
        diag = t12[:, 0, 0:256].rearrange("p (a b) -> p a b", a=2)
        smask = t12[:, 1, 0:128]

        pq = [ps("pq%d" % i, [128, 1024], F32) for i in range(4)]
        pqb = [pq[0].bitcast(BF16), pq[1].bitcast(BF16)]

        def bankb(b):
            return pqb[b // 2][:, (b % 2) * 1024:(b % 2) * 1024 + 512]

        def bank(b):
            return pq[b // 2][:, (b % 2) * 512:(b % 2 + 1) * 512]

        def bk(b):
            return [("pb", b)]

        P = Prog(nc)
        state = {"st": 0, "bc": 0, "gu": 0, "stb": 0, "pt": 0, "ot": 0, "pts": 0}

        def stcol():
            state["st"] = (state["st"] + 1) % 96
            c = state["st"]
            return stt[:, c:c + 1], ("stt", c)

        def pool_ms(t_ap, val, key):
            P.op("pool", lambda e: e.memset(t_ap, val), writes=[key])

        pool_ms(ident[:], 0.0, "ident")
        P.op("pool", lambda e: e.affine_select(out=ident[:], in_=ident[:], pattern=[[-1, 128]], compare_op=ALU.not_equal, fill=1.0, base=0, channel_multiplier=1), reads=["ident"], writes=["ident"])
        pool_ms(identf[:], 0.0, "identf")
        P.op("pool", lambda e: e.affine_select(out=identf[:], in_=identf[:], pattern=[[-1, 128]], compare_op=ALU.not_equal, fill=1.0, base=0, channel_multiplier=1), reads=["identf"], writes=["identf"])
        pool_ms(onesf[:], 1.0, "onesf")
        pool_ms(mhalf[:], -0.5, "mhalf")
        pool_ms(onesb[:], 1.0, "onesb")
        pool_ms(mask01[:], 1.0, "mask01")
        P.op("pool", lambda e: e.affine_select(out=mask01[:], in_=mask01[:], pattern=[[1, 128]], compare_op=ALU.is_ge, fill=0.0, base=0, channel_multiplier=-1), reads=["mask01"], writes=["mask01"])
        pool_ms(utri[:], 1.0, "utri")
        P.op("pool", lambda e: e.affine_select(out=utri[:], in_=utri[:], pattern=[[1, 128]], compare_op=ALU.is_ge, fill=0.0, base=0, channel_multiplier=-1), reads=["utri"], writes=["utri"])
        pool_ms(sel127[:], 0.0, "sel127")
        P.op("pool", lambda e: e.affine_select(out=sel127[:], in_=sel127[:], pattern=[[0, 128]], compare_op=ALU.not_equal, fill=1.0, base=-127, channel_multiplier=1), reads=["sel127"], writes=["sel127"])
        pool_ms(sel63[:], 0.0, "sel63")
        P.op("pool", lambda e: e.affine_select(out=sel63[:], in_=sel63[:], pattern=[[0, 128]], compare_op=ALU.not_equal, fill=1.0, base=-63, channel_multiplier=1), reads=["sel63"], writes=["sel63"])
        pool_ms(smask, 1.0, "t2")
        P.op("pool", lambda e: e.memset(t12[64:128, 1, 0:64], 0.0), reads=["t2"], writes=["t2"])
        pool_ms(QT[:], 0.0, "QTinit")
        pool_ms(roff[:], 0.0, "roffinit")
        VAf = VA[:].rearrange("p a b c -> p (a b) c")
        pool_ms(VAf[:, :, 64:65], 1.0, "VAones")
        pool_ms(OTs[:], 0.0, "OTsinit")

        def ld(dst, src, key):
            P.dma("sp", ("small", str(key)), lambda e: e.dma_start(out=dst, in_=src), writes=[key])

        ld(ccol[:], c_col_d, "ccol")
        ld(bada[:], b_ada_d, "bada")
        ld(gpre[:], gpre_d, "gpre")
        ld(gpost[:], gpost_d, "gpost")
        ld(bc4[:], bc_d, "bc4")
        ld(sguw[:, 0, :].rearrange("p (h i) -> p h i", h=8), sguwT_d, ("tmp", 0))
        ld(sgub[:], sgub_d, "sgub")
        ld(bfbc[:], bf_d, "bfbc")

        chunks = []
        loaded = {"n": 0}

        def ring_slot(idx):
            return idx % NRING

        def rkey(idx):
            return ("ring", idx % NRING)

        def need(idx):
            lim = min(idx + NRING - 1, len(chunks) - 1)
            while loaded["n"] <= lim:
                m = loaded["n"]
                srcs, dst_fn = chunks[m]
                slot = ring_slot(m)
                P.dma("pool", ("ring", slot), lambda e, srcs=srcs, dst_fn=dst_fn, slot=slot: e.dma_start(out=dst_fn(ring[slot]), in_=srcs), writes=[rkey(m)])
                loaded["n"] += 1

        def v8(rt):
            return rt[:].rearrange("p (a b) -> p a b", a=8)

        def v2(rt):
            return rt[:].rearrange("p (a b) -> p a b", a=2)

        def add_cols_chunk(w_ap, c0, ncols):
            src = w_ap.rearrange("(kc p) n -> p kc n", p=128)[:, :, c0:c0 + ncols]
            chunks.append((src, (lambda rt, ncols=ncols: v8(rt)[:, :, 0:ncols])))
            return len(chunks) - 1

        def add_rows_chunk(w_ap, c):
            src = w_ap.rearrange("(fc p) n -> p fc n", p=128)[:, 2 * c:2 * c + 2, :]
            chunks.append((src, (lambda rt: v2(rt))))
            return len(chunks) - 1

        ada_idx = [add_cols_chunk(w_ada_d, 256 * j, 256) for j in range(36)]
        sched = []
        for t in range(ntiles):
            e = {}
            for f, nm in ((0, "f1"), (1, "f2")):
                gl = []
                for c in range(11):
                    gi = add_cols_chunk(wg_d[f], 256 * c, 256)
                    ui = add_cols_chunk(wu_d[f], 256 * c, 256)
                    gl.append((gi, ui))
                dl = [add_rows_chunk(wd_d[f], c) for c in range(11)]
                e[nm] = (gl, dl)
                if f == 0:
                    e["in"] = {}
                    for c in (4, 5, 6, 7, 8, 9):
                        e["in"][c] = add_cols_chunk(win_d, 256 * c, 256)
                    e["in"][10] = add_cols_chunk(win_d, 2560, 8)
                    for c in (0, 1, 2, 3):
                        e["in"][c] = add_cols_chunk(win_d, 256 * c, 256)
                    e["out"] = [add_cols_chunk(wout_d, 256 * c, 256) for c in range(4)]
            sched.append(e)

        P.op("act", lambda e: e.activation(out=scb[:], in_=ccol[:], func=AF.Silu), reads=["ccol"], writes=["scb"])
        pm = bank(1)
        for j in range(36):
            need(ada_idx[j])
            sl = ring[ring_slot(ada_idx[j])]
            for cc in range(2):
                m = 2 * j + cc
                for kc in range(8):
                    P.op("pe", lambda e, sl=sl, cc=cc, m=m, kc=kc: e.matmul(pm[:, m:m + 1], lhsT=v8(sl)[:, kc, cc * 128:(cc + 1) * 128], rhs=scb[:, kc:kc + 1], start=(kc == 0), stop=(kc == 7)),
                         reads=[rkey(ada_idx[j]), "scb"], writes=bk(1))
        P.op("dve", lambda e: e.tensor_tensor(out=modT[:], in0=pm[:, 0:72], in1=bada[:], op=ALU.add), reads=bk(1) + ["bada"], writes=["modT"])
        for s in range(3):
            P.op("dve", lambda e, s=s: e.scalar_tensor_tensor(out=Acol[:, s * 8:(s + 1) * 8], in0=modT[:, (3 * s + 1) * 8:(3 * s + 2) * 8], scalar=1.0, in1=gpre[:, s * 8:(s + 1) * 8], op0=ALU.add, op1=ALU.mult),
                 reads=["modT", "gpre"], writes=["Acol0"])
            P.op("dve", lambda e, s=s: e.scalar_tensor_tensor(out=gcol[:, s * 8:(s + 1) * 8], in0=modT[:, (3 * s + 2) * 8:(3 * s + 3) * 8], scalar=1.0, in1=gpost[:, s * 8:(s + 1) * 8], op0=ALU.add, op1=ALU.mult),
                 reads=["modT", "gpost"], writes=["gcol0"])
        P.op("dve", lambda e: e.tensor_scalar(out=Acol[:], in0=Acol[:], scalar1=32.0, scalar2=None, op0=ALU.mult), reads=["Acol0"], writes=["Acol"])
        for s in range(3):
            coef = (0.5, 1.0, 0.5)[s] * 32.0
            P.op("dve", lambda e, s=s, coef=coef: e.tensor_scalar(out=gcol[:, s * 8:(s + 1) * 8], in0=gcol[:, s * 8:(s + 1) * 8], scalar1=coef, scalar2=None, op0=ALU.mult), reads=["gcol0"], writes=[("gcol", s)])
        for s in range(3):
            for dc in range(8):
                dsl = dc % 2
                b = 2 + (dc // 4)
                P.op("dve", lambda e, s=s, dc=dc, dsl=dsl: e.tensor_scalar(out=diag[:, dsl, :], in0=identf[:], scalar1=gcol[:, s * 8 + dc:s * 8 + dc + 1], scalar2=None, op0=ALU.mult),
                     reads=["identf", ("gcol", s)], writes=["t1"])
                P.op("pe", lambda e, dc=dc, dsl=dsl, b=b: e.matmul(bank(b)[:, (dc % 4) * 128:(dc % 4 + 1) * 128], lhsT=onesf[:], rhs=diag[:, dsl, :], start=True, stop=True),
                     reads=["onesf", "t1"], writes=bk(b))
            P.op("act", lambda e, s=s: e.activation(out=Gt[:, s, :], in_=pq[1][:, :], func=AF.Copy), reads=bk(2) + bk(3), writes=[("Gt", s)])
        P.op("dve", lambda e: e.tensor_scalar(out=bc4[:, 2:4, :], in0=bc4[:, 2:4, :], scalar1=float(np.sqrt(512.0)), scalar2=None, op0=ALU.mult), reads=["bc4"], writes=["bc4"])
        for h in range(8):
            P.op("dve", lambda e, h=h: e.tensor_tensor(out=wmT[:, h, :], in0=sguw[:, 0, h * 128:(h + 1) * 128], in1=smask, op=ALU.mult), reads=[("tmp", 0), "t2"], writes=["wmT"])

        def rstd_from(ss_ap, ss_key, n):
            a2, k2 = stcol()
            P.op("dve", lambda e: e.tensor_scalar(out=a2, in0=ss_ap, scalar1=float(n * EPS), scalar2=None, op0=ALU.add), reads=[ss_key], writes=[k2])
            a3, k3 = stcol()
            P.op("pool", lambda e: e.tensor_tensor(out=a3, in0=a2, in1=mhalf[:], op=ALU.pow), reads=[k2, "mhalf"], writes=[k3])
            return a3, k3

        def load_x(t):
            for r in range(4):
                P.dma("sp", ("xt", r), lambda e, r=r: e.dma_start(out=xt[:, r, :], in_=x_d[t * T + r * 128:t * T + (r + 1) * 128, :]), writes=[("xt", r)])

        def transposes_to(src_tile, src_keyfn, dst, dst_keyfn, evac):
            def region(kc):
                b = kc // 2
                o = (b % 2) * 1024 + (kc % 2) * 512
                return pqb[b // 2][:, o:o + 512]
            for kc in range(8):
                slot = kc % 4
                pt = bankb(slot).rearrange("p (r c) -> p r c", r=4)
                for r in range(4):
                    P.op("pe", lambda e, pt=pt, r=r, kc=kc: e.transpose(out=pt[:, r, :], in_=src_tile[:, r, kc * 128:(kc + 1) * 128], identity=ident[:]),
                         reads=[src_keyfn(r), "ident"], writes=bk(slot))
                evac(kc, bankb(slot), ("pb", slot))

        def xpre(r):
            return (zuv[:, r, :], ("zuv", r)) if r < 2 else (tmp[:, r - 2, :], ("tmp", r - 2))

        def prenorm_stats(r, src=None):
            xs, xk = src if src is not None else (xt[:, r, :], ("xt", r))
            a1, k1 = stcol()
            P.op("act", lambda e, r=r, a1=a1: e.activation(out=xn[:, r, :], in_=xs, func=AF.Square, accum_out=a1), reads=[xk], writes=[k1, ("xn", r)])
            a3, k3 = rstd_from(a1, k1, D)
            P.op("dve", lambda e, r=r, a3=a3: e.tensor_scalar(out=xn[:, r, :], in0=xs, scalar1=a3, scalar2=None, op0=ALU.mult), reads=[xk, k3], writes=[("xn", r)])

        def prenorm(s):

            def evac(kc, src, skey):
                if kc % 2 == 0:
                    P.op("act", lambda e: e.activation(out=hT[:, kc, :], in_=src, func=AF.Identity, scale=Acol[:, s * 8 + kc:s * 8 + kc + 1], bias=modT[:, 3 * s * 8 + kc:3 * s * 8 + kc + 1]),
                         reads=[skey, "Acol", "modT"], writes=[("hT", kc)])
                else:
                    P.op("dve", lambda e: e.tensor_scalar(out=hT[:, kc, :], in0=src, scalar1=Acol[:, s * 8 + kc:s * 8 + kc + 1], scalar2=modT[:, 3 * s * 8 + kc:3 * s * 8 + kc + 1], op0=ALU.mult, op1=ALU.add),
                         reads=[skey, "Acol", "modT"], writes=[("hT", kc)])
            transposes_to(xn, lambda r: ("xn", r), hT, None, evac)

        def post_a(ybanks, yap):
            a1, k1 = stcol()
            P.op("act", lambda e: e.activation(out=sg[:].rearrange("p a b -> p (a b)"), in_=yap, func=AF.Square, accum_out=a1), reads=ybanks, writes=[k1, ("sg", 0), ("sg", 1)])
            return rstd_from(a1, k1, D)

        def post_z(s, ybanks, yap):
            P.op("dve", lambda e: e.tensor_tensor(out=yap, in0=yap, in1=Gt[:, s, :], op=ALU.mult), reads=ybanks + [("Gt", s)], writes=ybanks)

        def post(s, ybanks, yap, r, last, t, ak=None, zdone=False, stats=True):
            a3, k3 = ak if ak is not None else post_a(ybanks, yap)
            if not zdone:
                post_z(s, ybanks, yap)
            P.op("dve", lambda e: e.scalar_tensor_tensor(out=xt[:, r, :], in0=yap, scalar=a3, in1=xt[:, r, :], op0=ALU.mult, op1=ALU.add), reads=ybanks + [k3, ("xt", r)], writes=[("xt", r)])
            if last:
                o = P.dma("sp", ("ost", r), lambda e: e.dma_start(out=out_d[t * T + r * 128:t * T + (r + 1) * 128, :], in_=xt[:, r, :]), reads=[("xt", r)], writes=[("outd", t, r)])
                finals.append(o)
                if t + 1 < ntiles:
                    xs, xk = xpre(r)
                    P.dma("sp", ("xt", r), lambda e: e.dma_start(out=xt[:, r, :], in_=xs), reads=[xk], writes=[("xt", r)])
            if stats and not last:
                prenorm_stats(r)

        finals = []

        def ffn(s, f, t, e_sched):
            gl, dl = e_sched
            prenorm(s)
            state["gu"] = 0
            for c in range(11):
                gi, ui = gl[c]
                need(gi)
                if s == 2 and t + 1 < ntiles and c == 0:
                    for r in range(4):
                        xs, xk = xpre(r)
                        P.dma("sp", ("xpre", r), lambda e, r=r, xs=xs: e.dma_start(out=xs, in_=x_d[(t + 1) * T + r * 128:(t + 1) * T + (r + 1) * 128, :]), writes=[xk])
                if s == 2 and t + 1 < ntiles and c == 4:
                    for r in range(4):
                        prenorm_stats(r, src=xpre(r))
                for cc in range(2):
                    fc = 2 * c + cc
                    bG, bU = ((4, 5), (6, 7), (0, 1), (2, 3))[state["gu"] % 4]
                    state["gu"] += 1
                    for (wi, bb) in ((gi, bG), (ui, bU)):
                        sl = ring[ring_slot(wi)]
                        for kc in range(8):
                            P.op("pe", lambda e, sl=sl, kc=kc, cc=cc, bb=bb: e.matmul(bank(bb), lhsT=v8(sl)[:, kc, cc * 128:(cc + 1) * 128], rhs=hT[:, kc, :], start=(kc == 0), stop=(kc == 7)),
                                 reads=[rkey(wi), ("hT", kc)], writes=bk(bb))
                    par = fc % 2
                    P.op("act", lambda e, bG=bG, par=par: e.activation(out=sg[:, par, :], in_=bank(bG), func=AF.Silu), reads=bk(bG), writes=[("sg", par)])
                    P.op("dve", lambda e, bU=bU, par=par, fc=fc: e.tensor_tensor(out=hid[:, fc, :], in0=sg[:, par, :], in1=bank(bU), op=ALU.mult), reads=bk(bU) + [("sg", par)], writes=[("hid", fc)])
            yk = lambda r: bk(2 * r) + bk(2 * r + 1)
            NTAIL = 3

            def down_mm(di, c, rlist):
                sl = ring[ring_slot(di)]
                for cc in range(2):
                    fc = 2 * c + cc
                    for r in rlist:
                        for half in range(2):
                            P.op("pe", lambda e, sl=sl, cc=cc, fc=fc, r=r, half=half: e.matmul(pq[r][:, half * 512:(half + 1) * 512], lhsT=hid[:, fc, r * 128:(r + 1) * 128], rhs=v2(sl)[:, cc, half * 512:(half + 1) * 512], start=(fc == 0), stop=(fc == NFC - 1)),
                                 reads=[rkey(di), ("hid", fc)], writes=bk(2 * r + half))

            for c in range(11 - NTAIL):
                need(dl[c])
                down_mm(dl[c], c, (0, 1, 2, 3))
            need(dl[11 - NTAIL])
            for r in range(4):
                for c in range(11 - NTAIL, 11):
                    down_mm(dl[c], c, (r,))
                ak = post_a(yk(r), pq[r][:, :])
                post_z(s, yk(r), pq[r][:, :])
                post(s, yk(r), pq[r][:, :], r, last=(s == 2), t=t, ak=ak, zdone=True, stats=False)
                if s != 2 and r >= 1:
                    prenorm_stats(r - 1)
            if s != 2:
                prenorm_stats(3)

        def sgu_a(r, zs):
            v = zuv[:, zs, 512:1024]
            zk = ("zuv", zs)
            P.op("dve", lambda e: e.bn_stats(out=bst[:, zs, :], in_=v), reads=[zk], writes=[("bst", zs)])
            P.op("dve", lambda e: e.bn_aggr(out=mv[:, zs, :], in_=bst[:, zs, :]), reads=[("bst", zs)], writes=[("mv", zs)])
            a2, k2 = stcol()
            P.op("dve", lambda e: e.tensor_scalar(out=a2, in0=mv[:, zs, 1:2], scalar1=EPS, scalar2=None, op0=ALU.add), reads=[("mv", zs)], writes=[k2])
            a3, k3 = stcol()
            P.op("pool", lambda e: e.tensor_tensor(out=a3, in0=a2, in1=mhalf[:], op=ALU.pow), reads=[k2, "mhalf"], writes=[k3])
            t1 = t12[:, 0, :]
            P.op("dve", lambda e: e.tensor_scalar(out=t1, in0=v, scalar1=mv[:, zs, 0:1], scalar2=a3, op0=ALU.subtract, op1=ALU.mult), reads=[zk, ("mv", zs), k3], writes=["t1"])
            P.op("pool", lambda e: e.tensor_tensor(out=t1, in0=t1, in1=bc4[:, 0, :], op=ALU.mult), reads=["t1", "bc4"], writes=["t1"])
            P.op("pool", lambda e: e.tensor_tensor(out=vnb[:, zs, :], in0=t1, in1=bc4[:, 1, :], op=ALU.add), reads=["t1", "bc4"], writes=[("vnb", zs)])

        def sgu_b(r, zs):
            u = zuv[:, zs, 0:512]
            zk = ("zuv", zs)
            sb_ = 1
            for h in range(8):
                P.op("pe", lambda e, h=h: e.matmul(bank(sb_)[:, h * 64:(h + 1) * 64], lhsT=wmT[:, h, :], rhs=vnb[:, zs, h * 64:(h + 1) * 64], start=True, stop=True), reads=["wmT", ("vnb", zs)], writes=bk(sb_))
            t2 = t12[:, 1, :]
            P.op("dve", lambda e: e.tensor_tensor(out=t2.rearrange("p (h c) -> p h c", h=8), in0=bank(sb_).rearrange("p (h c) -> p h c", h=8), in1=sgub[:, :].unsqueeze(2).to_broadcast([128, 8, 64]), op=ALU.add),
                 reads=bk(sb_) + ["sgub"], writes=["t2"])
            P.op("dve", lambda e: e.tensor_tensor(out=t2, in0=t2, in1=u, op=ALU.mult), reads=["t2", zk], writes=["t2"])
            a1, k1 = stcol()
            P.op("act", lambda e: e.activation(out=xn[:, r, 0:512], in_=t2, func=AF.Square, accum_out=a1), reads=["t2"], writes=[k1, ("xn", r)])
            r3, rk3 = rstd_from(a1, k1, 512)
            P.op("dve", lambda e: e.scalar_tensor_tensor(out=xn[:, r, 0:512], in0=t2, scalar=r3, in1=bc4[:, 2, :], op0=ALU.mult, op1=ALU.mult), reads=["t2", rk3, "bc4"], writes=[("xn", r)])

        def build_roff(t, h):
            rsl = h % 2
            rp = bank(0)[0:1, :]
            P.op("pe", lambda e, h=h, rp=rp: e.matmul(rp, lhsT=sel63[:, 0:1], rhs=cn[:, 4 * t:4 * t + 4, h].unsqueeze(2).to_broadcast([128, 4, 128]), start=True, stop=True),
                 reads=["sel63"] + [("cn", 4 * t + i) for i in range(4)], writes=bk(0))
            P.op("dve", lambda e, rsl=rsl, rp=rp: e.tensor_scalar(out=roff[0:1, rsl, :], in0=rp, scalar1=-1.0, scalar2=None, op0=ALU.mult), reads=bk(0) + ["roffinit"], writes=[("roff", rsl)])

        def attention(t, hooks=None):
            jmax = 4 * t + 3
            fin = {"f": None}
            state["stb"] = 0
            for h in range(8):
                p, bp = h // 2, 64 * (h % 2)
                ob = 2 + (h % 2)
                oacc = bank(ob)
                pend = []

                def do_pv(j, qlo, pslot):
                    P.op("pe", lambda e, j=j, qlo=qlo, pslot=pslot, h=h, oacc=oacc, jm=jmax: e.matmul(oacc[0:65, qlo:512], lhsT=VA[:, j, h, 0:65], rhs=PT[:, pslot, qlo:512], start=(j == 0), stop=(j == jm)),
                         reads=[("VA", j), ("PT", pslot)], writes=bk(ob))

                rsl = h % 2
                if h > 0 and jmax < 2:
                    build_roff(t, h)
                for j in range(jmax + 1):
                    qlo = 0 if j < 4 * t else (j - 4 * t) * 128
                    sbk = (4, 5, 6, 7)[state["stb"] % 4]
                    state["stb"] += 1
                    P.op("pe", lambda e, j=j, qlo=qlo, sbk=sbk, h=h, p=p: e.matmul(bank(sbk)[:, qlo:512], lhsT=KT[:, p, j * 128:(j + 1) * 128], rhs=QT[:, h, qlo:512], start=True, stop=False),
                         reads=[("KT", j // 4), ("QT", p)], writes=bk(sbk))
                    P.op("pe", lambda e, qlo=qlo, sbk=sbk, rsl=rsl: e.matmul(bank(sbk)[:, qlo:512], lhsT=onesb[:], rhs=roff[:, rsl, qlo:512], start=False, stop=True),
                         reads=["onesb", ("roff", rsl)], writes=bk(sbk))
                    if len(pend) >= 2:
                        do_pv(*pend.pop(0))
                    pslot = state["pts"] % 3
                    state["pts"] += 1
                    P.op("act", lambda e, j=j, h=h, qlo=qlo, sbk=sbk, pslot=pslot: e.activation(out=PT[:, pslot, qlo:512], in_=bank(sbk)[:, qlo:512], func=AF.Exp, bias=cn[:, j, h:h + 1], scale=1.0),
                         reads=bk(sbk) + [("cn", j)], writes=[("PT", pslot)])
                    if j >= 4 * t:
                        i = j - 4 * t
                        P.op("pool", lambda e, i=i, pslot=pslot: e.tensor_tensor(out=PT[:, pslot, i * 128:(i + 1) * 128], in0=PT[:, pslot, i * 128:(i + 1) * 128], in1=mask01[:], op=ALU.mult),
                             reads=[("PT", pslot), "mask01"], writes=[("PT", pslot)])
                    pend.append((j, qlo, pslot))
                    if j == 1 and h + 1 < 8:
                        build_roff(t, h + 1)
                    if j == 1:
                        if fin["f"] is not None:
                            fin["f"]()
                            fin["f"] = None
                        if hooks is not None and h in hooks:
                            hooks[h]()
                while pend:
                    do_pv(*pend.pop(0))
                osl = h % 2
                P.op("dve", lambda e, osl=osl, oacc=oacc: e.tensor_copy(out=OTs[0:65, osl, :], in_=oacc[0:65, :]), reads=bk(ob), writes=[("OTs", osl)])
                fin["f"] = (lambda h=h, ob=ob, oacc=oacc: finish_head(h, ob, oacc))
            fin["f"]()

        def finish_head(h, ob, oacc):
            if True:
                osl = h % 2
                tb = bank(1)
                for r in range(4):
                    P.op("pe", lambda e, r=r, osl=osl: e.transpose(out=tb[:, r * 66:(r + 1) * 66], in_=OTs[0:66, osl, r * 128:(r + 1) * 128], identity=identf[0:66, 0:66]), reads=[("OTs", osl), "identf"], writes=bk(1))
                tbv = tb[:, 0:264].rearrange("p (r c) -> p r c", c=66)
                P.op("dve", lambda e, osl=osl, tbv=tbv: e.reciprocal(out=rl[:, osl, :], in_=tbv[:, :, 64:65].rearrange("p r c -> p (r c)")), reads=bk(1), writes=[("rl", osl)])
                ybv = tmp[:].rearrange("p s (r c) -> p (s r) c", r=2)
                P.op("dve", lambda e, osl=osl, tbv=tbv, ybv=ybv, h=h: e.tensor_tensor(out=ybv[:, :, h * 64:(h + 1) * 64], in0=tbv[:, :, 0:64], in1=rl[:, osl, :].unsqueeze(2).to_broadcast([128, 4, 64]), op=ALU.mult),
                     reads=bk(1) + [("rl", osl)], writes=[("tmp", 0), ("tmp", 1)])

        def mixer(t, e):
            s = 1
            win = e["in"]
            wout = e["out"]
            prenorm(s)
            for p in range(8):
                wi = win[4 + p // 2]
                cc = p % 2
                need(wi)
                sl = ring[ring_slot(wi)]
                b = 4 + (p % 4)
                for kc in range(8):
                    P.op("pe", lambda e_, sl=sl, kc=kc, cc=cc, b=b: e_.matmul(bank(b), lhsT=v8(sl)[:, kc, cc * 128:(cc + 1) * 128], rhs=hT[:, kc, :], start=(kc == 0), stop=(kc == 7)),
                         reads=[rkey(wi), ("hT", kc)], writes=bk(b))
                if p < 4:
                    P.op("act", lambda e_, p=p, b=b: e_.activation(out=QT[0:64, 2 * p, :], in_=bank(b)[0:64, :], func=AF.Copy, scale=0.125), reads=bk(b) + ["QTinit"], writes=[("QT", p)])
                    P.op("dve", lambda e_, p=p, b=b: e_.tensor_scalar(out=QT[64:128, 2 * p + 1, :], in0=bank(b)[64:128, :], scalar1=0.125, scalar2=None, op0=ALU.mult), reads=bk(b) + ["QTinit"], writes=[("QT", p)])
                else:
                    P.op("dve", lambda e_, p=p, b=b: e_.tensor_copy(out=KT[:, p - 4, t * T:(t + 1) * T], in_=bank(b)), reads=bk(b), writes=[("KT", t)])
            need(win[8])
            for r in range(4):
                b = 4 + (r % 4)
                bi = 4 * t + r
                for ch in range(2):
                    wi = win[8 + ch]
                    sl = ring[ring_slot(wi)]
                    for kc in range(8):
                        P.op("pe", lambda e_, sl=sl, kc=kc, ch=ch, r=r, b=b: e_.matmul(bank(b)[:, ch * 256:(ch + 1) * 256], lhsT=hT[:, kc, r * 128:(r + 1) * 128], rhs=v8(sl)[:, kc, :], start=(kc == 0), stop=(kc == 7)),
                             reads=[rkey(wi), ("hT", kc)], writes=bk(b))
                P.op("dve", lambda e_, b=b, bi=bi: e_.tensor_copy(out=VA[:, bi, :, 0:64], in_=bank(b).rearrange("p (h c) -> p h c", h=8)), reads=bk(b) + ["VAones"], writes=[("VA", bi)])
            wi = win[10]
            need(wi)
            sl = ring[ring_slot(wi)]
            fb = bank(1)
            for r in range(4):
                for kc in range(8):
                    P.op("pe", lambda e_, kc=kc, r=r, sl=sl: e_.matmul(fb[:, r * 8:(r + 1) * 8], lhsT=hT[:, kc, r * 128:(r + 1) * 128], rhs=v8(sl)[:, kc, 0:8], start=(kc == 0), stop=(kc == 7)),
                         reads=[rkey(wi), ("hT", kc)], writes=bk(1))
            P.op("dve", lambda e_: e_.tensor_tensor(out=fl[:, 0, :].rearrange("p (r h) -> p r h", r=4), in0=fb[:, 0:32].rearrange("p (r h) -> p r h", r=4), in1=bfbc[:, :].unsqueeze(1).to_broadcast([128, 4, 8]), op=ALU.add),
                 reads=bk(1) + ["bfbc"], writes=["fl0"])
            P.op("act", lambda e_: e_.activation(out=fl[:, 1, :], in_=fl[:, 0, :], func=AF.Exp, scale=-1.0), reads=["fl0"], writes=["fl1"])
            P.op("act", lambda e_: e_.activation(out=fl[:, 2, :], in_=fl[:, 1, :], func=AF.Ln, bias=1.0, scale=1.0), reads=["fl1"], writes=["fl2"])
            for r in range(4):
                bi = 4 * t + r
                cb = bank(1)[:, 32:40]
                P.op("pe", lambda e_, r=r, bi=bi, cb=cb: e_.matmul(cb, lhsT=utri[:], rhs=fl[:, 2, r * 8:(r + 1) * 8], start=True, stop=(bi == 0)), reads=["utri", "fl2"], writes=bk(1))
                if bi > 0:
                    P.op("pe", lambda e_, bi=bi, cb=cb: e_.matmul(cb, lhsT=sel127[:], rhs=cn[:, bi - 1, :], start=False, stop=True), reads=["sel127", ("cn", bi - 1)], writes=bk(1))
                P.op("dve", lambda e_, bi=bi, cb=cb: e_.tensor_copy(out=cn[:, bi, :], in_=cb), reads=bk(1), writes=[("cn", bi)])
            build_roff(t, 0)
            need(win[0])

            def zmm(r):
                zs = r % 2
                zb = 2 + zs
                for c in range(4):
                    sl = ring[ring_slot(win[c])]
                    for kc in range(8):
                        P.op("pe", lambda e_, sl=sl, c=c, kc=kc, r=r, zb=zb: e_.matmul(pq[zb][:, c * 256:(c + 1) * 256], lhsT=hT[:, kc, r * 128:(r + 1) * 128], rhs=v8(sl)[:, kc, :], start=(kc == 0), stop=(kc == 7)),
                             reads=[rkey(win[c]), ("hT", kc)], writes=bk(2 * zb + c // 2))

            def zgelu(r):
                zs = r % 2
                zb = 2 + zs
                P.op("act", lambda e_, zs=zs, zb=zb: e_.activation(out=zuv[:, zs, :], in_=pq[zb][:, :], func=AF.Gelu_apprx_tanh), reads=bk(2 * zb) + bk(2 * zb + 1), writes=[("zuv", zs)])
                sgu_a(r, zs)

            zmm(0)
            zgelu(0)
            zmm(1)
            zgelu(1)
            zmm(2)
            sgu_b(0, 0)
            zgelu(2)
            zmm(3)
            sgu_b(1, 1)
            zgelu(3)
            need(wout[0])
            attention(t, hooks={0: (lambda: sgu_b(2, 0)), 1: (lambda: sgu_b(3, 1))})
            ybv = tmp[:].rearrange("p s (r c) -> p (s r) c", r=2)
            for r in range(4):
                a1, k1 = stcol()
                P.op("act", lambda e_, r=r, a1=a1: e_.activation(out=xn[:, r, 512:1024], in_=ybv[:, r, :], func=AF.Square, accum_out=a1), reads=[("tmp", r // 2)], writes=[k1, ("xn", r)])
                r3, rk3 = rstd_from(a1, k1, 512)
                P.op("dve", lambda e_, r=r, r3=r3: e_.scalar_tensor_tensor(out=xn[:, r, 512:1024], in0=ybv[:, r, :], scalar=r3, in1=bc4[:, 3, :], op0=ALU.mult, op1=ALU.mult), reads=[("tmp", r // 2), rk3, "bc4"], writes=[("xn", r)])
            def evac(kc, src, skey):
                if kc % 2 == 0:
                    P.op("act", lambda e_: e_.activation(out=hT[:, kc, :], in_=src, func=AF.Copy), reads=[skey], writes=[("hT", kc)])
                else:
                    P.op("dve", lambda e_: e_.tensor_copy(out=hT[:, kc, :], in_=src), reads=[skey], writes=[("hT", kc)])
            transposes_to(xn, lambda r: ("xn", r), hT, None, evac)
            need(wout[0])
            for r in range(4):
                yb_ = 2 + (r % 2)
                for c in range(4):
                    sl = ring[ring_slot(wout[c])]
                    for kc in range(8):
                        P.op("pe", lambda e_, sl=sl, c=c, kc=kc, r=r, yb_=yb_: e_.matmul(pq[yb_][:, c * 256:(c + 1) * 256], lhsT=hT[:, kc, r * 128:(r + 1) * 128], rhs=v8(sl)[:, kc, :], start=(kc == 0), stop=(kc == 7)),
                             reads=[rkey(wout[c]), ("hT", kc)], writes=bk(2 * yb_ + c // 2))
                post(s, bk(2 * yb_) + bk(2 * yb_ + 1), pq[yb_][:, :], r, last=False, t=t)

        load_x(0)
        for r in range(4):
            prenorm_stats(r)
        for t in range(ntiles):
            e = sched[t]
            ffn(0, 0, t, e["f1"])
            mixer(t, e)
            ffn(2, 1, t, e["f2"])
        P.emit(final_waits=finals)
    return nc


_CACHE = {}


def kernel(x, c, w_ada, b_ada, norm_pre_g, norm_post_g, ffn_w_gate, ffn_w_up, ffn_w_down,
           w_in, sgu_ln_g, sgu_ln_b, sgu_w, sgu_b, fox_b_f, gnorm_a_g, gnorm_b_g, w_out):
    f32 = np.float32
    x = np.asarray(x, f32)
    B = x.shape[0]
    A = lambda a: np.ascontiguousarray(np.asarray(a, f32))
    if "nc" not in _CACHE:
        _CACHE["nc"] = build_program(NTILES_RUN)
    nc = _CACHE["nc"]
    col = lambda v, n: A(np.asarray(v, f32).reshape(n, 128).T)
    shared = {
        "w_ada": A(w_ada[0]),
        "b_ada_col": col(b_ada[0], 72),
        "gpre_col": col(norm_pre_g[0].reshape(-1), 24),
        "gpost_col": col(norm_post_g[0].reshape(-1), 24),
        "wg": A(ffn_w_gate[0]), "wu": A(ffn_w_up[0]), "wd": A(ffn_w_down[0]),
        "w_in": A(w_in[0]), "w_out": A(w_out[0]),
        "bc4": A(np.broadcast_to(np.stack([np.asarray(sgu_ln_g[0], f32), np.asarray(sgu_ln_b[0], f32), np.asarray(gnorm_a_g[0], f32), np.asarray(gnorm_b_g[0], f32)])[None], (128, 4, 512))),
        "sguwT": A(np.transpose(np.asarray(sgu_w[0], f32), (2, 0, 1))),
        "sgub_col": A(np.asarray(sgu_b[0], f32).T),
        "bf_bc": A(np.broadcast_to(np.asarray(fox_b_f[0], f32)[None], (128, 8))),
    }
    in_maps = []
    for b in range(B):
        m = dict(shared)
        m["x"] = A(x[b])
        m["c_col"] = col(np.asarray(c, f32)[b], 8)
        in_maps.append(m)
    res = run_bass_kernel_spmd(nc, in_maps, core_ids=list(range(B)))
    return np.stack([np.asarray(r["out"], f32) for r in res.results], axis=0)
```

```python
import numpy as np
import concourse.bass as bass
import concourse.mybir as mybir

F32 = mybir.dt.float32
BF16 = mybir.dt.bfloat16
ALU = mybir.AluOpType
AF = mybir.ActivationFunctionType
AX = mybir.AxisListType

ENGS = ("pe", "act", "dve", "pool", "sp")


class Op:
    __slots__ = ("eng", "fn", "deps", "signal", "semval", "is_dma", "dkey", "idx", "gidx")

    def __init__(self, eng, fn, is_dma=False, dkey=None):
        self.eng = eng
        self.fn = fn
        self.deps = []
        self.signal = False
        self.semval = 0
        self.is_dma = is_dma
        self.dkey = dkey
        self.idx = -1
        self.gidx = -1


class Prog:
    def __init__(self, nc):
        self.nc = nc
        self.ops = {e: [] for e in ENGS}
        self.last_w = {}
        self.readers = {}
        self.dma_cnt = {}
        self.n = 0

    def _add(self, op, reads, writes):
        op.gidx = self.n
        self.n += 1
        deps = []
        for r in reads:
            w = self.last_w.get(r)
            if w is not None:
                deps.append((w, "raw"))
        for r in writes:
            w = self.last_w.get(r)
            if w is not None:
                deps.append((w, "waw"))
            for rd in self.readers.get(r, ()):
                deps.append((rd, "war"))
        seen = set()
        for d, kind in deps:
            if d is op or id(d) in seen:
                continue
            if d.eng == op.eng and not d.is_dma and not op.is_dma:
                if op.eng == "pe":
                    continue
            if d.eng == op.eng and d.is_dma and op.is_dma and kind == "waw" and d.dkey == op.dkey:
                continue
            seen.add(id(d))
            op.deps.append(d)
            d.signal = True
        for r in reads:
            self.readers.setdefault(r, []).append(op)
        for r in writes:
            self.last_w[r] = op
            self.readers[r] = []
        self.ops[op.eng].append(op)
        return op

    def op(self, eng, fn, reads=(), writes=()):
        return self._add(Op(eng, fn), reads, writes)

    def dma(self, queue, dkey, fn, reads=(), writes=()):
        o = Op(queue, fn, is_dma=True, dkey=dkey)
        o.signal = True
        return self._add(o, reads, writes)

    def emit(self, final_waits=()):
        nc = self.nc
        for e in ENGS:
            c = 0
            for o in self.ops[e]:
                if o.is_dma:
                    self.dma_cnt[o.dkey] = self.dma_cnt.get(o.dkey, 0) + 16
                    o.semval = self.dma_cnt[o.dkey]
                elif o.signal:
                    c += 1
                    o.semval = c
        dkeys = sorted(self.dma_cnt.keys(), key=str)
        import contextlib
        with contextlib.ExitStack() as st:
            esem = {e: st.enter_context(nc.semaphore("s_" + e)) for e in ENGS}
            dsem = {k: st.enter_context(nc.semaphore("d_%d" % i)) for i, k in enumerate(dkeys)}
            block = st.enter_context(nc.Block())
            engobj = {}

            def run(ename, eng):
                waited = {}
                for o in self.ops[ename]:
                    for d in o.deps:
                        if d.is_dma:
                            key = ("d", d.dkey)
                            sem = dsem[d.dkey]
                        else:
                            key = ("e", d.eng)
                            sem = esem[d.eng]
                        if waited.get(key, 0) >= d.semval:
                            continue
                        waited[key] = d.semval
                        eng.wait_ge(sem, d.semval)
                    ins = o.fn(eng)
                    if o.is_dma:
                        ins.then_inc(dsem[o.dkey], 16)
                    elif o.signal:
                        ins.then_inc(esem[ename], 1)
                if ename == "sp":
                    for o in final_waits:
                        if o.is_dma:
                            eng.wait_ge(dsem[o.dkey], o.semval)
                        else:
                            eng.wait_ge(esem[o.eng], o.semval)

            @block.tensor
            def _(eng):
                run("pe", eng)

            @block.scalar
            def _(eng):
                run("act", eng)

            @block.vector
            def _(eng):
                run("dve", eng)

            @block.gpsimd
            def _(eng):
                run("pool", eng)

            @block.sync
            def _(eng):
                run("sp", eng)


from contextlib import ExitStack
from concourse.bass_utils import run_bass_kernel_spmd

D = 1024
S = 4096
T = 512
NT = S // T
FF = 2816
NFC = FF // 128
NIN = 2568
EPS = 1e-6
NRING = 5
NTILES_RUN = NT


def build_program(ntiles=NT):
    nc = bass.Bass("TRN2", target_bir_lowering=False)
    dt_in = lambda name, shape: nc.dram_tensor(name, shape, F32, kind="ExternalInput").ap()
    x_d = dt_in("x", [S, D])
    out_d = nc.dram_tensor("out", [S, D], F32, kind="ExternalOutput").ap()
    c_col_d = dt_in("c_col", [128, 8])
    w_ada_d = dt_in("w_ada", [D, 9 * D])
    b_ada_d = dt_in("b_ada_col", [128, 72])
    gpre_d = dt_in("gpre_col", [128, 24])
    gpost_d = dt_in("gpost_col", [128, 24])
    wg_d = dt_in("wg", [2, D, FF])
    wu_d = dt_in("wu", [2, D, FF])
    wd_d = dt_in("wd", [2, FF, D])
    win_d = dt_in("w_in", [D, NIN])
    wout_d = dt_in("w_out", [D, D])
    bc_d = dt_in("bc4", [128, 4, 512])
    sguwT_d = dt_in("sguwT", [128, 8, 128])
    sgub_d = dt_in("sgub_col", [128, 8])
    bf_d = dt_in("bf_bc", [128, 8])

    with ExitStack() as st:
        def sb(name, shape, dt):
            return st.enter_context(nc.sbuf_tensor(name, shape, dt))

        def ps(name, shape, dt):
            return st.enter_context(nc.psum_tensor(name, shape, dt))

        KT = sb("KT", [128, 4, S], BF16)
        VA = sb("VA", [128, S // 128, 8, 65], BF16)
        xt = sb("xt", [128, 4, D], F32)
        tmp = sb("tmp", [128, 2, D], F32)
        xn = sb("xn", [128, 4, D], BF16)
        hT = sb("hT", [128, 8, T], BF16)
        hid = sb("hid", [128, NFC, T], BF16)
        ring = [sb("ring%d" % i, [128, 2048], BF16) for i in range(NRING)]
        Gt = sb("Gt", [128, 3, D], F32)
        sg = sb("sg", [128, 2, T], BF16)
        zuv = sb("zuv", [128, 2, D], F32)
        t12 = sb("t12", [128, 2, 512], F32)
        vnb = sb("vnb", [128, 2, 512], BF16)
        QT = sb("QT", [128, 8, T], BF16)
        PT = sb("PT", [128, 3, T], BF16)
        OTs = sb("OTs", [128, 2, T], F32)
        wmT = sb("wmT", [128, 8, 128], BF16)
        bc4 = sb("bc4s", [128, 4, 512], F32)
        ident = sb("ident", [128, 128], BF16)
        identf = sb("identf", [128, 128], F32)
        onesf = sb("onesf", [128, 128], F32)
        mask01 = sb("mask01", [128, 128], BF16)
        utri = sb("utri", [128, 128], F32)
        sel127 = sb("sel127", [128, 128], F32)
        sel63 = sb("sel63", [128, 128], F32)
        mhalf = sb("mhalf", [128, 1], F32)
        ccol = sb("ccol", [128, 8], F32)
        scb = sb("scb", [128, 8], BF16)
        bada = sb("bada", [128, 72], F32)
        modT = sb("modT", [128, 72], F32)
        gpre = sb("gpre", [128, 24], F32)
        gpost = sb("gpost", [128, 24], F32)
        Acol = sb("Acol", [128, 24], F32)
        gcol = sb("gcol", [128, 24], F32)
        sgub = sb("sgub", [128, 8], F32)
        bfbc = sb("bfbc", [128, 8], F32)
        stt = sb("stt", [128, 96], F32)
        cn = sb("cn", [128, S // 128, 8], F32)
        roff = sb("roff", [128, 2, T], BF16)
        onesb = sb("onesb", [128, 128], BF16)
        fl = sb("fl", [128, 3, 32], F32)
        bst = sb("bst", [128, 2, 6], F32)
        mv = sb("mv", [128, 2, 2], F32)
        rl = sb("rl", [128, 2, 4], F32)
        sguw = tmp
        diag = t12[:, 0, 0:256].rearrange("p (a b) -> p a b", a=2)
        smask = t12[:, 1, 0:128]

        pq = [ps("pq%d" % i, [128, 1024], F32) for i in range(4)]
        pqb = [pq[0].bitcast(BF16), pq[1].bitcast(BF16)]

        def bankb(b):
            return pqb[b // 2][:, (b % 2) * 1024:(b % 2) * 1024 + 512]

        def bank(b):
            return pq[b // 2][:, (b % 2) * 512:(b % 2 + 1) * 512]

        def bk(b):
            return [("pb", b)]

        P = Prog(nc)
        state = {"st": 0, "bc": 0, "gu": 0, "stb": 0, "pt": 0, "ot": 0, "pts": 0}

        def stcol():
            state["st"] = (state["st"] + 1) % 96
            c = state["st"]
            return stt[:, c:c + 1], ("stt", c)

        def pool_ms(t_ap, val, key):
            P.op("pool", lambda e: e.memset(t_ap, val), writes=[key])

        pool_ms(ident[:], 0.0, "ident")
        P.op("pool", lambda e: e.affine_select(out=ident[:], in_=ident[:], pattern=[[-1, 128]], compare_op=ALU.not_equal, fill=1.0, base=0, channel_multiplier=1), reads=["ident"], writes=["ident"])
        pool_ms(identf[:], 0.0, "identf")
        P.op("pool", lambda e: e.affine_select(out=identf[:], in_=identf[:], pattern=[[-1, 128]], compare_op=ALU.not_equal, fill=1.0, base=0, channel_multiplier=1), reads=["identf"], writes=["identf"])
        pool_ms(onesf[:], 1.0, "onesf")
        pool_ms(mhalf[:], -0.5, "mhalf")
        pool_ms(onesb[:], 1.0, "onesb")
        pool_ms(mask01[:], 1.0, "mask01")
        P.op("pool", lambda e: e.affine_select(out=mask01[:], in_=mask01[:], pattern=[[1, 128]], compare_op=ALU.is_ge, fill=0.0, base=0, channel_multiplier=-1), reads=["mask01"], writes=["mask01"])
        pool_ms(utri[:], 1.0, "utri")
        P.op("pool", lambda e: e.affine_select(out=utri[:], in_=utri[:], pattern=[[1, 128]], compare_op=ALU.is_ge, fill=0.0, base=0, channel_multiplier=-1), reads=["utri"], writes=["utri"])
        pool_ms(sel127[:], 0.0, "sel127")
        P.op("pool", lambda e: e.affine_select(out=sel127[:], in_=sel127[:], pattern=[[0, 128]], compare_op=ALU.not_equal, fill=1.0, base=-127, channel_multiplier=1), reads=["sel127"], writes=["sel127"])
        pool_ms(sel63[:], 0.0, "sel63")
        P.op("pool", lambda e: e.affine_select(out=sel63[:], in_=sel63[:], pattern=[[0, 128]], compare_op=ALU.not_equal, fill=1.0, base=-63, channel_multiplier=1), reads=["sel63"], writes=["sel63"])
        pool_ms(smask, 1.0, "t2")
        P.op("pool", lambda e: e.memset(t12[64:128, 1, 0:64], 0.0), reads=["t2"], writes=["t2"])
        pool_ms(QT[:], 0.0, "QTinit")
        pool_ms(roff[:], 0.0, "roffinit")
        VAf = VA[:].rearrange("p a b c -> p (a b) c")
        pool_ms(VAf[:, :, 64:65], 1.0, "VAones")
        pool_ms(OTs[:], 0.0, "OTsinit")

        def ld(dst, src, key):
            P.dma("sp", ("small", str(key)), lambda e: e.dma_start(out=dst, in_=src), writes=[key])

        ld(ccol[:], c_col_d, "ccol")
        ld(bada[:], b_ada_d, "bada")
        ld(gpre[:], gpre_d, "gpre")
        ld(gpost[:], gpost_d, "gpost")
        ld(bc4[:], bc_d, "bc4")
        ld(sguw[:, 0, :].rearrange("p (h i) -> p h i", h=8), sguwT_d, ("tmp", 0))
        ld(sgub[:], sgub_d, "sgub")
        ld(bfbc[:], bf_d, "bfbc")

        chunks = []
        loaded = {"n": 0}

        def ring_slot(idx):
            return idx % NRING

        def rkey(idx):
            return ("ring", idx % NRING)

        def need(idx):
            lim = min(idx + NRING - 1, len(chunks) - 1)
            while loaded["n"] <= lim:
                m = loaded["n"]
                srcs, dst_fn = chunks[m]
                slot = ring_slot(m)
                P.dma("pool", ("ring", slot), lambda e, srcs=srcs, dst_fn=dst_fn, slot=slot: e.dma_start(out=dst_fn(ring[slot]), in_=srcs), writes=[rkey(m)])
                loaded["n"] += 1

        def v8(rt):
            return rt[:].rearrange("p (a b) -> p a b", a=8)

        def v2(rt):
            return rt[:].rearrange("p (a b) -> p a b", a=2)

        def add_cols_chunk(w_ap, c0, ncols):
            src = w_ap.rearrange("(kc p) n -> p kc n", p=128)[:, :, c0:c0 + ncols]
            chunks.append((src, (lambda rt, ncols=ncols: v8(rt)[:, :, 0:ncols])))
            return len(chunks) - 1

        def add_rows_chunk(w_ap, c):
            src = w_ap.rearrange("(fc p) n -> p fc n", p=128)[:, 2 * c:2 * c + 2, :]
            chunks.append((src, (lambda rt: v2(rt))))
            return len(chunks) - 1

        ada_idx = [add_cols_chunk(w_ada_d, 256 * j, 256) for j in range(36)]
        sched = []
        for t in range(ntiles):
            e = {}
            for f, nm in ((0, "f1"), (1, "f2")):
                gl = []
                for c in range(11):
                    gi = add_cols_chunk(wg_d[f], 256 * c, 256)
                    ui = add_cols_chunk(wu_d[f], 256 * c, 256)
                    gl.append((gi, ui))
                dl = [add_rows_chunk(wd_d[f], c) for c in range(11)]
                e[nm] = (gl, dl)
                if f == 0:
                    e["in"] = {}
                    for c in (4, 5, 6, 7, 8, 9):
                        e["in"][c] = add_cols_chunk(win_d, 256 * c, 256)
                    e["in"][10] = add_cols_chunk(win_d, 2560, 8)
                    for c in (0, 1, 2, 3):
                        e["in"][c] = add_cols_chunk(win_d, 256 * c, 256)
                    e["out"] = [add_cols_chunk(wout_d, 256 * c, 256) for c in range(4)]
            sched.append(e)

        P.op("act", lambda e: e.activation(out=scb[:], in_=ccol[:], func=AF.Silu), reads=["ccol"], writes=["scb"])
        pm = bank(1)
        for j in range(36):
            need(ada_idx[j])
            sl = ring[ring_slot(ada_idx[j])]
            for cc in range(2):
                m = 2 * j + cc
                for kc in range(8):
                    P.op("pe", lambda e, sl=sl, cc=cc, m=m, kc=kc: e.matmul(pm[:, m:m + 1], lhsT=v8(sl)[:, kc, cc * 128:(cc + 1) * 128], rhs=scb[:, kc:kc + 1], start=(kc == 0), stop=(kc == 7)),
                         reads=[rkey(ada_idx[j]), "scb"], writes=bk(1))
        P.op("dve", lambda e: e.tensor_tensor(out=modT[:], in0=pm[:, 0:72], in1=bada[:], op=ALU.add), reads=bk(1) + ["bada"], writes=["modT"])
        for s in range(3):
            P.op("dve", lambda e, s=s: e.scalar_tensor_tensor(out=Acol[:, s * 8:(s + 1) * 8], in0=modT[:, (3 * s + 1) * 8:(3 * s + 2) * 8], scalar=1.0, in1=gpre[:, s * 8:(s + 1) * 8], op0=ALU.add, op1=ALU.mult),
                 reads=["modT", "gpre"], writes=["Acol0"])
            P.op("dve", lambda e, s=s: e.scalar_tensor_tensor(out=gcol[:, s * 8:(s + 1) * 8], in0=modT[:, (3 * s + 2) * 8:(3 * s + 3) * 8], scalar=1.0, in1=gpost[:, s * 8:(s + 1) * 8], op0=ALU.add, op1=ALU.mult),
                 reads=["modT", "gpost"], writes=["gcol0"])
        P.op("dve", lambda e: e.tensor_scalar(out=Acol[:], in0=Acol[:], scalar1=32.0, scalar2=None, op0=ALU.mult), reads=["Acol0"], writes=["Acol"])
        for s in range(3):
            coef = (0.5, 1.0, 0.5)[s] * 32.0
            P.op("dve", lambda e, s=s, coef=coef: e.tensor_scalar(out=gcol[:, s * 8:(s + 1) * 8], in0=gcol[:, s * 8:(s + 1) * 8], scalar1=coef, scalar2=None, op0=ALU.mult), reads=["gcol0"], writes=[("gcol", s)])
        for s in range(3):
            for dc in range(8):
                dsl = dc % 2
                b = 2 + (dc // 4)
                P.op("dve", lambda e, s=s, dc=dc, dsl=dsl: e.tensor_scalar(out=diag[:, dsl, :], in0=identf[:], scalar1=gcol[:, s * 8 + dc:s * 8 + dc + 1], scalar2=None, op0=ALU.mult),
                     reads=["identf", ("gcol", s)], writes=["t1"])
                P.op("pe", lambda e, dc=dc, dsl=dsl, b=b: e.matmul(bank(b)[:, (dc % 4) * 128:(dc % 4 + 1) * 128], lhsT=onesf[:], rhs=diag[:, dsl, :], start=True, stop=True),
                     reads=["onesf", "t1"], writes=bk(b))
            P.op("act", lambda e, s=s: e.activation(out=Gt[:, s, :], in_=pq[1][:, :], func=AF.Copy), reads=bk(2) + bk(3), writes=[("Gt", s)])
        P.op("dve", lambda e: e.tensor_scalar(out=bc4[:, 2:4, :], in0=bc4[:, 2:4, :], scalar1=float(np.sqrt(512.0)), scalar2=None, op0=ALU.mult), reads=["bc4"], writes=["bc4"])
        for h in range(8):
            P.op("dve", lambda e, h=h: e.tensor_tensor(out=wmT[:, h, :], in0=sguw[:, 0, h * 128:(h + 1) * 128], in1=smask, op=ALU.mult), reads=[("tmp", 0), "t2"], writes=["wmT"])

        def rstd_from(ss_ap, ss_key, n):
            a2, k2 = stcol()
            P.op("dve", lambda e: e.tensor_scalar(out=a2, in0=ss_ap, scalar1=float(n * EPS), scalar2=None, op0=ALU.add), reads=[ss_key], writes=[k2])
            a3, k3 = stcol()
            P.op("pool", lambda e: e.tensor_tensor(out=a3, in0=a2, in1=mhalf[:], op=ALU.pow), reads=[k2, "mhalf"], writes=[k3])
            return a3, k3

        def load_x(t):
            for r in range(4):
                P.dma("sp", ("xt", r), lambda e, r=r: e.dma_start(out=xt[:, r, :], in_=x_d[t * T + r * 128:t * T + (r + 1) * 128, :]), writes=[("xt", r)])

        def transposes_to(src_tile, src_keyfn, dst, dst_keyfn, evac):
            def region(kc):
                b = kc // 2
                o = (b % 2) * 1024 + (kc % 2) * 512
                return pqb[b // 2][:, o:o + 512]
            for kc in range(8):
                slot = kc % 4
                pt = bankb(slot).rearrange("p (r c) -> p r c", r=4)
                for r in range(4):
                    P.op("pe", lambda e, pt=pt, r=r, kc=kc: e.transpose(out=pt[:, r, :], in_=src_tile[:, r, kc * 128:(kc + 1) * 128], identity=ident[:]),
                         reads=[src_keyfn(r), "ident"], writes=bk(slot))
                evac(kc, bankb(slot), ("pb", slot))

        def xpre(r):
            return (zuv[:, r, :], ("zuv", r)) if r < 2 else (tmp[:, r - 2, :], ("tmp", r - 2))

        def prenorm_stats(r, src=None):
            xs, xk = src if src is not None else (xt[:, r, :], ("xt", r))
            a1, k1 = stcol()
            P.op("act", lambda e, r=r, a1=a1: e.activation(out=xn[:, r, :], in_=xs, func=AF.Square, accum_out=a1), reads=[xk], writes=[k1, ("xn", r)])
            a3, k3 = rstd_from(a1, k1, D)
            P.op("dve", lambda e, r=r, a3=a3: e.tensor_scalar(out=xn[:, r, :], in0=xs, scalar1=a3, scalar2=None, op0=ALU.mult), reads=[xk, k3], writes=[("xn", r)])

        def prenorm(s):

            def evac(kc, src, skey):
                if kc in (0, 3, 6):
                    P.op("act", lambda e: e.activation(out=hT[:, kc, :], in_=src, func=AF.Identity, scale=Acol[:, s * 8 + kc:s * 8 + kc + 1], bias=modT[:, 3 * s * 8 + kc:3 * s * 8 + kc + 1]),
                         reads=[skey, "Acol", "modT"], writes=[("hT", kc)])
                else:
                    P.op("dve", lambda e: e.tensor_scalar(out=hT[:, kc, :], in0=src, scalar1=Acol[:, s * 8 + kc:s * 8 + kc + 1], scalar2=modT[:, 3 * s * 8 + kc:3 * s * 8 + kc + 1], op0=ALU.mult, op1=ALU.add),
                         reads=[skey, "Acol", "modT"], writes=[("hT", kc)])
            transposes_to(xn, lambda r: ("xn", r), hT, None, evac)

        def post_a(ybanks, yap):
            a1, k1 = stcol()
            P.op("act", lambda e: e.activation(out=sg[:].rearrange("p a b -> p (a b)"), in_=yap, func=AF.Square, accum_out=a1), reads=ybanks, writes=[k1, ("sg", 0), ("sg", 1)])
            return rstd_from(a1, k1, D)

        def post_z(s, ybanks, yap):
            P.op("dve", lambda e: e.tensor_tensor(out=yap, in0=yap, in1=Gt[:, s, :], op=ALU.mult), reads=ybanks + [("Gt", s)], writes=ybanks)

        def post(s, ybanks, yap, r, last, t, ak=None, zdone=False, stats=True):
            a3, k3 = ak if ak is not None else post_a(ybanks, yap)
            if not zdone:
                post_z(s, ybanks, yap)
            P.op("dve", lambda e: e.scalar_tensor_tensor(out=xt[:, r, :], in0=yap, scalar=a3, in1=xt[:, r, :], op0=ALU.mult, op1=ALU.add), reads=ybanks + [k3, ("xt", r)], writes=[("xt", r)])
            if last:
                o = P.dma("sp", ("ost", r), lambda e: e.dma_start(out=out_d[t * T + r * 128:t * T + (r + 1) * 128, :], in_=xt[:, r, :]), reads=[("xt", r)], writes=[("outd", t, r)])
                finals.append(o)
                if t + 1 < ntiles:
                    xs, xk = xpre(r)
                    P.dma("sp", ("xt", r), lambda e: e.dma_start(out=xt[:, r, :], in_=xs), reads=[xk], writes=[("xt", r)])
            if stats and not last:
                prenorm_stats(r)

        finals = []

        def ffn(s, f, t, e_sched):
            gl, dl = e_sched
            prenorm(s)
            state["gu"] = 0
            for c in range(11):
                gi, ui = gl[c]
                need(gi)
                if s == 2 and t + 1 < ntiles and c == 0:
                    for r in range(4):
                        xs, xk = xpre(r)
                        P.dma("sp", ("xpre", r), lambda e, r=r, xs=xs: e.dma_start(out=xs, in_=x_d[(t + 1) * T + r * 128:(t + 1) * T + (r + 1) * 128, :]), writes=[xk])
                if s == 2 and t + 1 < ntiles and c == 4:
                    for r in range(4):
                        prenorm_stats(r, src=xpre(r))
                for cc in range(2):
                    fc = 2 * c + cc
                    bG, bU = ((4, 5), (6, 7), (0, 1), (2, 3))[state["gu"] % 4]
                    state["gu"] += 1
                    for (wi, bb) in ((gi, bG), (ui, bU)):
                        sl = ring[ring_slot(wi)]
                        for kc in range(8):
                            P.op("pe", lambda e, sl=sl, kc=kc, cc=cc, bb=bb: e.matmul(bank(bb), lhsT=v8(sl)[:, kc, cc * 128:(cc + 1) * 128], rhs=hT[:, kc, :], start=(kc == 0), stop=(kc == 7)),
                                 reads=[rkey(wi), ("hT", kc)], writes=bk(bb))
                    par = fc % 2
                    P.op("act", lambda e, bG=bG, par=par: e.activation(out=sg[:, par, :], in_=bank(bG), func=AF.Silu), reads=bk(bG), writes=[("sg", par)])
                    P.op("dve", lambda e, bU=bU, par=par, fc=fc: e.tensor_tensor(out=hid[:, fc, :], in0=sg[:, par, :], in1=bank(bU), op=ALU.mult), reads=bk(bU) + [("sg", par)], writes=[("hid", fc)])
            yk = lambda r: bk(2 * r) + bk(2 * r + 1)
            NTAIL = 3

            def down_mm(di, c, rlist):
                sl = ring[ring_slot(di)]
                for cc in range(2):
                    fc = 2 * c + cc
                    for r in rlist:
                        for half in range(2):
                            P.op("pe", lambda e, sl=sl, cc=cc, fc=fc, r=r, half=half: e.matmul(pq[r][:, half * 512:(half + 1) * 512], lhsT=hid[:, fc, r * 128:(r + 1) * 128], rhs=v2(sl)[:, cc, half * 512:(half + 1) * 512], start=(fc == 0), stop=(fc == NFC - 1)),
                                 reads=[rkey(di), ("hid", fc)], writes=bk(2 * r + half))

            for c in range(11 - NTAIL):
                need(dl[c])
                down_mm(dl[c], c, (0, 1, 2, 3))
            need(dl[11 - NTAIL])
            for r in range(4):
                for c in range(11 - NTAIL, 11):
                    down_mm(dl[c], c, (r,))
                ak = post_a(yk(r), pq[r][:, :])
                post_z(s, yk(r), pq[r][:, :])
                post(s, yk(r), pq[r][:, :], r, last=(s == 2), t=t, ak=ak, zdone=True, stats=False)
                if s != 2 and r >= 1:
                    prenorm_stats(r - 1)
            if s != 2:
                prenorm_stats(3)

        def sgu_a(r, zs):
            v = zuv[:, zs, 512:1024]
            zk = ("zuv", zs)
            P.op("dve", lambda e: e.bn_stats(out=bst[:, zs, :], in_=v), reads=[zk], writes=[("bst", zs)])
            P.op("dve", lambda e: e.bn_aggr(out=mv[:, zs, :], in_=bst[:, zs, :]), reads=[("bst", zs)], writes=[("mv", zs)])
            a2, k2 = stcol()
            P.op("dve", lambda e: e.tensor_scalar(out=a2, in0=mv[:, zs, 1:2], scalar1=EPS, scalar2=None, op0=ALU.add), reads=[("mv", zs)], writes=[k2])
            a3, k3 = stcol()
            P.op("pool", lambda e: e.tensor_tensor(out=a3, in0=a2, in1=mhalf[:], op=ALU.pow), reads=[k2, "mhalf"], writes=[k3])
            t1 = t12[:, 0, :]
            P.op("dve", lambda e: e.tensor_scalar(out=t1, in0=v, scalar1=mv[:, zs, 0:1], scalar2=a3, op0=ALU.subtract, op1=ALU.mult), reads=[zk, ("mv", zs), k3], writes=["t1"])
            P.op("pool", lambda e: e.tensor_tensor(out=t1, in0=t1, in1=bc4[:, 0, :], op=ALU.mult), reads=["t1", "bc4"], writes=["t1"])
            P.op("pool", lambda e: e.tensor_tensor(out=vnb[:, zs, :], in0=t1, in1=bc4[:, 1, :], op=ALU.add), reads=["t1", "bc4"], writes=[("vnb", zs)])

        def sgu_b(r, zs):
            u = zuv[:, zs, 0:512]
            zk = ("zuv", zs)
            sb_ = 1
            for h in range(8):
                P.op("pe", lambda e, h=h: e.matmul(bank(sb_)[:, h * 64:(h + 1) * 64], lhsT=wmT[:, h, :], rhs=vnb[:, zs, h * 64:(h + 1) * 64], start=True, stop=True), reads=["wmT", ("vnb", zs)], writes=bk(sb_))
            t2 = t12[:, 1, :]
            P.op("dve", lambda e: e.tensor_tensor(out=t2.rearrange("p (h c) -> p h c", h=8), in0=bank(sb_).rearrange("p (h c) -> p h c", h=8), in1=sgub[:, :].unsqueeze(2).to_broadcast([128, 8, 64]), op=ALU.add),
                 reads=bk(sb_) + ["sgub"], writes=["t2"])
            P.op("dve", lambda e: e.tensor_tensor(out=t2, in0=t2, in1=u, op=ALU.mult), reads=["t2", zk], writes=["t2"])
            a1, k1 = stcol()
            P.op("act", lambda e: e.activation(out=xn[:, r, 0:512], in_=t2, func=AF.Square, accum_out=a1), reads=["t2"], writes=[k1, ("xn", r)])
            r3, rk3 = rstd_from(a1, k1, 512)
            P.op("dve", lambda e: e.scalar_tensor_tensor(out=xn[:, r, 0:512], in0=t2, scalar=r3, in1=bc4[:, 2, :], op0=ALU.mult, op1=ALU.mult), reads=["t2", rk3, "bc4"], writes=[("xn", r)])

        def build_roff(t, h):
            rsl = h % 2
            rp = bank(0)[0:1, :]
            P.op("pe", lambda e, h=h, rp=rp: e.matmul(rp, lhsT=sel63[:, 0:1], rhs=cn[:, 4 * t:4 * t + 4, h].unsqueeze(2).to_broadcast([128, 4, 128]), start=True, stop=True),
                 reads=["sel63"] + [("cn", 4 * t + i) for i in range(4)], writes=bk(0))
            P.op("dve", lambda e, rsl=rsl, rp=rp: e.tensor_scalar(out=roff[0:1, rsl, :], in0=rp, scalar1=-1.0, scalar2=None, op0=ALU.mult), reads=bk(0) + ["roffinit"], writes=[("roff", rsl)])

        def attention(t, hooks=None):
            jmax = 4 * t + 3
            fin = {"f": None}
            state["stb"] = 0
            for h in range(8):
                p, bp = h // 2, 64 * (h % 2)
                ob = 2 + (h % 2)
                oacc = bank(ob)
                pend = []

                def do_pv(j, qlo, pslot):
                    P.op("pe", lambda e, j=j, qlo=qlo, pslot=pslot, h=h, oacc=oacc, jm=jmax: e.matmul(oacc[0:65, qlo:512], lhsT=VA[:, j, h, 0:65], rhs=PT[:, pslot, qlo:512], start=(j == 0), stop=(j == jm)),
                         reads=[("VA", j), ("PT", pslot)], writes=bk(ob))

                rsl = h % 2
                if h > 0 and jmax < 2:
                    build_roff(t, h)
                for j in range(jmax + 1):
                    qlo = 0 if j < 4 * t else (j - 4 * t) * 128
                    sbk = (4, 5, 6, 7)[state["stb"] % 4]
                    state["stb"] += 1
                    P.op("pe", lambda e, j=j, qlo=qlo, sbk=sbk, h=h, p=p: e.matmul(bank(sbk)[:, qlo:512], lhsT=KT[:, p, j * 128:(j + 1) * 128], rhs=QT[:, h, qlo:512], start=True, stop=False),
                         reads=[("KT", j // 4), ("QT", p)], writes=bk(sbk))
                    P.op("pe", lambda e, qlo=qlo, sbk=sbk, rsl=rsl: e.matmul(bank(sbk)[:, qlo:512], lhsT=onesb[:], rhs=roff[:, rsl, qlo:512], start=False, stop=True),
                         reads=["onesb", ("roff", rsl)], writes=bk(sbk))
                    if len(pend) >= 2:
                        do_pv(*pend.pop(0))
                    pslot = state["pts"] % 3
                    state["pts"] += 1
                    P.op("act", lambda e, j=j, h=h, qlo=qlo, sbk=sbk, pslot=pslot: e.activation(out=PT[:, pslot, qlo:512], in_=bank(sbk)[:, qlo:512], func=AF.Exp, bias=cn[:, j, h:h + 1], scale=1.0),
                         reads=bk(sbk) + [("cn", j)], writes=[("PT", pslot)])
                    if j >= 4 * t:
                        i = j - 4 * t
                        P.op("pool", lambda e, i=i, pslot=pslot: e.tensor_tensor(out=PT[:, pslot, i * 128:(i + 1) * 128], in0=PT[:, pslot, i * 128:(i + 1) * 128], in1=mask01[:], op=ALU.mult),
                             reads=[("PT", pslot), "mask01"], writes=[("PT", pslot)])
                    pend.append((j, qlo, pslot))
                    if j == 1 and h + 1 < 8:
                        build_roff(t, h + 1)
                    if j == 1:
                        if fin["f"] is not None:
                            fin["f"]()
                            fin["f"] = None
                        if hooks is not None and h in hooks:
                            hooks[h]()
                while pend:
                    do_pv(*pend.pop(0))
                osl = h % 2
                P.op("dve", lambda e, osl=osl, oacc=oacc: e.tensor_copy(out=OTs[0:65, osl, :], in_=oacc[0:65, :]), reads=bk(ob), writes=[("OTs", osl)])
                fin["f"] = (lambda h=h, ob=ob, oacc=oacc: finish_head(h, ob, oacc))
            fin["f"]()

        def finish_head(h, ob, oacc):
            if True:
                osl = h % 2
                tb = bank(1)
                for r in range(4):
                    P.op("pe", lambda e, r=r, osl=osl: e.transpose(out=tb[:, r * 66:(r + 1) * 66], in_=OTs[0:66, osl, r * 128:(r + 1) * 128], identity=identf[0:66, 0:66]), reads=[("OTs", osl), "identf"], writes=bk(1))
                tbv = tb[:, 0:264].rearrange("p (r c) -> p r c", c=66)
                P.op("dve", lambda e, osl=osl, tbv=tbv: e.reciprocal(out=rl[:, osl, :], in_=tbv[:, :, 64:65].rearrange("p r c -> p (r c)")), reads=bk(1), writes=[("rl", osl)])
                ybv = tmp[:].rearrange("p s (r c) -> p (s r) c", r=2)
                P.op("dve", lambda e, osl=osl, tbv=tbv, ybv=ybv, h=h: e.tensor_tensor(out=ybv[:, :, h * 64:(h + 1) * 64], in0=tbv[:, :, 0:64], in1=rl[:, osl, :].unsqueeze(2).to_broadcast([128, 4, 64]), op=ALU.mult),
                     reads=bk(1) + [("rl", osl)], writes=[("tmp", 0), ("tmp", 1)])

        def mixer(t, e):
            s = 1
            win = e["in"]
            wout = e["out"]
            prenorm(s)
            for p in range(8):
                wi = win[4 + p // 2]
                cc = p % 2
                need(wi)
                sl = ring[ring_slot(wi)]
                b = 2 + (p % 2)
                for kc in range(8):
                    P.op("pe", lambda e_, sl=sl, kc=kc, cc=cc, b=b: e_.matmul(bank(b), lhsT=v8(sl)[:, kc, cc * 128:(cc + 1) * 128], rhs=hT[:, kc, :], start=(kc == 0), stop=(kc == 7)),
                         reads=[rkey(wi), ("hT", kc)], writes=bk(b))
                if p < 4:
                    P.op("act", lambda e_, p=p, b=b: e_.activation(out=QT[0:64, 2 * p, :], in_=bank(b)[0:64, :], func=AF.Copy, scale=0.125), reads=bk(b) + ["QTinit"], writes=[("QT", p)])
                    P.op("dve", lambda e_, p=p, b=b: e_.tensor_scalar(out=QT[64:128, 2 * p + 1, :], in0=bank(b)[64:128, :], scalar1=0.125, scalar2=None, op0=ALU.mult), reads=bk(b) + ["QTinit"], writes=[("QT", p)])
                else:
                    P.op("dve", lambda e_, p=p, b=b: e_.tensor_copy(out=KT[:, p - 4, t * T:(t + 1) * T], in_=bank(b)), reads=bk(b), writes=[("KT", t)])
            need(win[8])
            for r in range(4):
                b = 2 + (r % 2)
                bi = 4 * t + r
                for ch in range(2):
                    wi = win[8 + ch]
                    sl = ring[ring_slot(wi)]
                    for kc in range(8):
                        P.op("pe", lambda e_, sl=sl, kc=kc, ch=ch, r=r, b=b: e_.matmul(bank(b)[:, ch * 256:(ch + 1) * 256], lhsT=hT[:, kc, r * 128:(r + 1) * 128], rhs=v8(sl)[:, kc, :], start=(kc == 0), stop=(kc == 7)),
                             reads=[rkey(wi), ("hT", kc)], writes=bk(b))
                P.op("dve", lambda e_, b=b, bi=bi: e_.tensor_copy(out=VA[:, bi, :, 0:64], in_=bank(b).rearrange("p (h c) -> p h c", h=8)), reads=bk(b) + ["VAones"], writes=[("VA", bi)])
            wi = win[10]
            need(wi)
            sl = ring[ring_slot(wi)]
            fb = bank(1)
            for r in range(4):
                for kc in range(8):
                    P.op("pe", lambda e_, kc=kc, r=r, sl=sl: e_.matmul(fb[:, r * 8:(r + 1) * 8], lhsT=hT[:, kc, r * 128:(r + 1) * 128], rhs=v8(sl)[:, kc, 0:8], start=(kc == 0), stop=(kc == 7)),
                         reads=[rkey(wi), ("hT", kc)], writes=bk(1))
            P.op("dve", lambda e_: e_.tensor_tensor(out=fl[:, 0, :].rearrange("p (r h) -> p r h", r=4), in0=fb[:, 0:32].rearrange("p (r h) -> p r h", r=4), in1=bfbc[:, :].unsqueeze(1).to_broadcast([128, 4, 8]), op=ALU.add),
                 reads=bk(1) + ["bfbc"], writes=["fl0"])
            P.op("act", lambda e_: e_.activation(out=fl[:, 1, :], in_=fl[:, 0, :], func=AF.Exp, scale=-1.0), reads=["fl0"], writes=["fl1"])
            P.op("act", lambda e_: e_.activation(out=fl[:, 2, :], in_=fl[:, 1, :], func=AF.Ln, bias=1.0, scale=1.0), reads=["fl1"], writes=["fl2"])
            for r in range(4):
                bi = 4 * t + r
                cb = bank(1)[:, 32:40]
                P.op("pe", lambda e_, r=r, bi=bi, cb=cb: e_.matmul(cb, lhsT=utri[:], rhs=fl[:, 2, r * 8:(r + 1) * 8], start=True, stop=(bi == 0)), reads=["utri", "fl2"], writes=bk(1))
                if bi > 0:
                    P.op("pe", lambda e_, bi=bi, cb=cb: e_.matmul(cb, lhsT=sel127[:], rhs=cn[:, bi - 1, :], start=False, stop=True), reads=["sel127", ("cn", bi - 1)], writes=bk(1))
                P.op("dve", lambda e_, bi=bi, cb=cb: e_.tensor_copy(out=cn[:, bi, :], in_=cb), reads=bk(1), writes=[("cn", bi)])
            build_roff(t, 0)
            need(win[0])

            def zmm(r):
                zs = r % 2
                zb = 2 + zs
                for c in range(4):
                    sl = ring[ring_slot(win[c])]
                    for kc in range(8):
                        P.op("pe", lambda e_, sl=sl, c=c, kc=kc, r=r, zb=zb: e_.matmul(pq[zb][:, c * 256:(c + 1) * 256], lhsT=hT[:, kc, r * 128:(r + 1) * 128], rhs=v8(sl)[:, kc, :], start=(kc == 0), stop=(kc == 7)),
                             reads=[rkey(win[c]), ("hT", kc)], writes=bk(2 * zb + c // 2))

            def zgelu(r):
                zs = r % 2
                zb = 2 + zs
                P.op("act", lambda e_, zs=zs, zb=zb: e_.activation(out=zuv[:, zs, :], in_=pq[zb][:, :], func=AF.Gelu_apprx_tanh), reads=bk(2 * zb) + bk(2 * zb + 1), writes=[("zuv", zs)])
                sgu_a(r, zs)

            zmm(0)
            zgelu(0)
            zmm(1)
            zgelu(1)
            zmm(2)
            sgu_b(0, 0)
            zgelu(2)
            zmm(3)
            sgu_b(1, 1)
            zgelu(3)
            need(wout[0])
            attention(t, hooks={0: (lambda: sgu_b(2, 0)), 1: (lambda: sgu_b(3, 1))})
            ybv = tmp[:].rearrange("p s (r c) -> p (s r) c", r=2)
            for r in range(4):
                a1, k1 = stcol()
                P.op("act", lambda e_, r=r, a1=a1: e_.activation(out=xn[:, r, 512:1024], in_=ybv[:, r, :], func=AF.Square, accum_out=a1), reads=[("tmp", r // 2)], writes=[k1, ("xn", r)])
                r3, rk3 = rstd_from(a1, k1, 512)
                P.op("dve", lambda e_, r=r, r3=r3: e_.scalar_tensor_tensor(out=xn[:, r, 512:1024], in0=ybv[:, r, :], scalar=r3, in1=bc4[:, 3, :], op0=ALU.mult, op1=ALU.mult), reads=[("tmp", r // 2), rk3, "bc4"], writes=[("xn", r)])
            def evac(kc, src, skey):
                if kc in (0, 3, 6):
                    P.op("act", lambda e_: e_.activation(out=hT[:, kc, :], in_=src, func=AF.Copy), reads=[skey], writes=[("hT", kc)])
                else:
                    P.op("dve", lambda e_: e_.tensor_copy(out=hT[:, kc, :], in_=src), reads=[skey], writes=[("hT", kc)])
            transposes_to(xn, lambda r: ("xn", r), hT, None, evac)
            need(wout[0])
            for r in range(4):
                yb_ = 2 + (r % 2)
                for c in range(4):
                    sl = ring[ring_slot(wout[c])]
                    for kc in range(8):
                        P.op("pe", lambda e_, sl=sl, c=c, kc=kc, r=r, yb_=yb_: e_.matmul(pq[yb_][:, c * 256:(c + 1) * 256], lhsT=hT[:, kc, r * 128:(r + 1) * 128], rhs=v8(sl)[:, kc, :], start=(kc == 0), stop=(kc == 7)),
                             reads=[rkey(wout[c]), ("hT", kc)], writes=bk(2 * yb_ + c // 2))
                post(s, bk(2 * yb_) + bk(2 * yb_ + 1), pq[yb_][:, :], r, last=False, t=t)

        load_x(0)
        for r in range(4):
            prenorm_stats(r)
        for t in range(ntiles):
            e = sched[t]
            ffn(0, 0, t, e["f1"])
            mixer(t, e)
            ffn(2, 1, t, e["f2"])
        P.emit(final_waits=finals)
    return nc


_CACHE = {}


def kernel(x, c, w_ada, b_ada, norm_pre_g, norm_post_g, ffn_w_gate, ffn_w_up, ffn_w_down,
           w_in, sgu_ln_g, sgu_ln_b, sgu_w, sgu_b, fox_b_f, gnorm_a_g, gnorm_b_g, w_out):
    f32 = np.float32
    x = np.asarray(x, f32)
    B = x.shape[0]
    A = lambda a: np.ascontiguousarray(np.asarray(a, f32))
    if "nc" not in _CACHE:
        _CACHE["nc"] = build_program(NTILES_RUN)
    nc = _CACHE["nc"]
    col = lambda v, n: A(np.asarray(v, f32).reshape(n, 128).T)
    shared = {
        "w_ada": A(w_ada[0]),
        "b_ada_col": col(b_ada[0], 72),
        "gpre_col": col(norm_pre_g[0].reshape(-1), 24),
        "gpost_col": col(norm_post_g[0].reshape(-1), 24),
        "wg": A(ffn_w_gate[0]), "wu": A(ffn_w_up[0]), "wd": A(ffn_w_down[0]),
        "w_in": A(w_in[0]), "w_out": A(w_out[0]),
        "bc4": A(np.broadcast_to(np.stack([np.asarray(sgu_ln_g[0], f32), np.asarray(sgu_ln_b[0], f32), np.asarray(gnorm_a_g[0], f32), np.asarray(gnorm_b_g[0], f32)])[None], (128, 4, 512))),
        "sguwT": A(np.transpose(np.asarray(sgu_w[0], f32), (2, 0, 1))),
        "sgub_col": A(np.asarray(sgu_b[0], f32).T),
        "bf_bc": A(np.broadcast_to(np.asarray(fox_b_f[0], f32)[None], (128, 8))),
    }
    in_maps = []
    for b in range(B):
        m = dict(shared)
        m["x"] = A(x[b])
        m["c_col"] = col(np.asarray(c, f32)[b], 8)
        in_maps.append(m)
    res = run_bass_kernel_spmd(nc, in_maps, core_ids=list(range(B)))
    return np.stack([np.asarray(r["out"], f32) for r in res.results], axis=0)
```
